# Optimizing a Trainium2 kernel written in Bass

```python
import math
import jax, jax.numpy as jnp
from jax import lax
import numpy as np

D_MODEL = 1024
BATCH = 2
SEQ = 8192
DEPTH = 4

N_A = DEPTH // 2
N_B = DEPTH - N_A
ALPHA = (2.0 * DEPTH) ** 0.25
BETA = (8.0 * DEPTH) ** -0.25
EPS = 1e-5
D_FF = ((8 * D_MODEL // 3 + 255) // 256) * 256
FFN_RES = 0.5
D_INNER = 2 * D_MODEL
M_HEAD_DIM = 64
M_HEADS = D_INNER // M_HEAD_DIM
M_GROUPS = 8
D_STATE = 128
D_CONV = 4
CHUNK = 128
CONV_DIM = D_INNER + 2 * M_GROUPS * D_STATE
IN_PROJ_DIM = D_INNER + CONV_DIM + M_HEADS
A_HEAD_DIM = 64
A_HEADS = D_MODEL // A_HEAD_DIM
BRANCHES = ((128, 1), (512, 4), (2048, 16))
N_BRANCH = len(BRANCHES)
A_WIDTH = A_HEADS * A_HEAD_DIM
Q_DIM = N_BRANCH * A_WIDTH
NUM_BUCKETS = 32
MAX_DISTANCE = 2048
NEG = -1e30

kernel_name = 'yoco_mamba2_dilated_attn_macaron_deepnorm'


def layer_norm(x, g, b):
    xf = x.astype(jnp.float32)
    mu = xf.mean(-1, keepdims=True)
    var = jnp.square(xf - mu).mean(-1, keepdims=True)
    return ((xf - mu) * lax.rsqrt(var + EPS) * g + b).astype(x.dtype)


def swiglu(x, w_in, w_out):
    gate, up = jnp.split(x @ w_in, 2, axis=-1)
    return (jax.nn.silu(gate) * up) @ w_out


def causal_depthwise_conv(u, w, b):
    out = lax.conv_general_dilated(u, w[:, None, :].astype(u.dtype), window_strides=(1,),
                                   padding=[(D_CONV - 1, 0)],
                                   dimension_numbers=('NWC', 'WIO', 'NWC'),
                                   feature_group_count=u.shape[-1])
    return out + b


def ssd_chunked(x, dt, A, Bm, Cm):
    Bz, S, H, P = x.shape
    G, N = Bm.shape[2], Bm.shape[3]
    J = H // G
    c, l = S // CHUNK, CHUNK
    x = x.astype(jnp.float32)
    xdt = (x * dt[..., None]).reshape(Bz, c, l, G, J, P)
    a_cum = jnp.cumsum((dt * A).reshape(Bz, c, l, G, J), axis=2)
    Bc = Bm.astype(jnp.float32).reshape(Bz, c, l, G, N)
    Cc = Cm.astype(jnp.float32).reshape(Bz, c, l, G, N)
    seg = a_cum[:, :, :, None] - a_cum[:, :, None, :]
    causal = jnp.tril(jnp.ones((l, l), dtype=bool))[None, None, :, :, None, None]
    decay = jnp.exp(jnp.where(causal, seg, -jnp.inf))
    cb = jnp.einsum('bclgn,bcsgn->bclsg', Cc, Bc)
    y_diag = jnp.einsum('bclsgj,bcsgjp->bclgjp', cb[..., None] * decay, xdt)
    decay_to_end = jnp.exp(a_cum[:, :, -1:] - a_cum)
    states = jnp.einsum('bclgn,bclgj,bclgjp->bcgjpn', Bc, decay_to_end, xdt)
    chunk_decay = jnp.exp(a_cum[:, :, -1])

    def step(h, inp):
        st, dec = inp
        return dec[..., None, None] * h + st, h

    h0 = jnp.zeros((Bz, G, J, P, N), jnp.float32)
    _, prev = lax.scan(step, h0, (jnp.moveaxis(states, 1, 0), jnp.moveaxis(chunk_decay, 1, 0)))
    prev = jnp.moveaxis(prev, 0, 1)
    y_off = jnp.einsum('bclgn,bcgjpn,bclgj->bclgjp', Cc, prev, jnp.exp(a_cum))
    return (y_diag + y_off).reshape(Bz, S, H, P)


def gated_rmsnorm(y, z, w):
    g = (y * jax.nn.silu(z)).astype(jnp.float32)
    g = g.reshape(*g.shape[:-1], M_GROUPS, -1)
    g = g * lax.rsqrt(jnp.mean(jnp.square(g), -1, keepdims=True) + EPS)
    return (g.reshape(y.shape) * w).astype(y.dtype)


def mamba2_mixer(x, w_in, conv_w, conv_b, dt_bias, a_log, d_skip, norm_w, w_out):
    Bz, S, _ = x.shape
    zxbcdt = x @ w_in
    z = zxbcdt[..., :D_INNER]
    xbc = zxbcdt[..., D_INNER:D_INNER + CONV_DIM]
    dt_raw = zxbcdt[..., D_INNER + CONV_DIM:]
    xbc = jax.nn.silu(causal_depthwise_conv(xbc, conv_w, conv_b))
    xs = xbc[..., :D_INNER].reshape(Bz, S, M_HEADS, M_HEAD_DIM)
    Bm = xbc[..., D_INNER:D_INNER + M_GROUPS * D_STATE].reshape(Bz, S, M_GROUPS, D_STATE)
    Cm = xbc[..., D_INNER + M_GROUPS * D_STATE:].reshape(Bz, S, M_GROUPS, D_STATE)
    dt = jax.nn.softplus(dt_raw.astype(jnp.float32) + dt_bias)
    A = -jnp.exp(a_log.astype(jnp.float32))
    y = ssd_chunked(xs, dt, A, Bm, Cm) + d_skip[:, None] * xs.astype(jnp.float32)
    y = y.reshape(Bz, S, D_INNER).astype(x.dtype)
    return gated_rmsnorm(y, z, norm_w) @ w_out


def t5_causal_bucket(dist):
    max_exact = NUM_BUCKETS // 2
    logv = jnp.log(jnp.maximum(dist, 1).astype(jnp.float32) / max_exact) / math.log(MAX_DISTANCE / max_exact)
    large = jnp.minimum(max_exact + (logv * (NUM_BUCKETS - max_exact)).astype(jnp.int32), NUM_BUCKETS - 1)
    return jnp.where(dist < max_exact, dist, large)


def dilated_branch(q, k, v, bias_hd, r, n):
    Bz, S, H, hd = q.shape
    L = S // r
    nb = -(-L // n)
    Lp = nb * n

    def strided(t):
        return t.reshape(Bz, L, r, H, hd).transpose(0, 2, 1, 3, 4).reshape(Bz * r, L, H, hd)

    def key_blocks(t):
        t = jnp.pad(strided(t), ((0, 0), (n, Lp - L), (0, 0), (0, 0))).reshape(Bz * r, nb + 1, n, H, hd)
        return jnp.concatenate([t[:, :-1], t[:, 1:]], axis=2)

    qs = jnp.pad(strided(q), ((0, 0), (0, Lp - L), (0, 0), (0, 0))).reshape(Bz * r, nb, n, H, hd)
    kb, vb = key_blocks(k), key_blocks(v)
    logits = jnp.einsum('zbihd,zbkhd->zbhik', qs, kb, preferred_element_type=jnp.float32) * (hd ** -0.5)
    qi = jnp.arange(n)[:, None]
    kk = jnp.arange(2 * n)[None, :]
    dist = n + qi - kk
    key_pos = (jnp.arange(nb) * n - n)[:, None, None] + kk[None]
    valid = (dist >= 0) & (dist <= n) & (key_pos >= 0)
    bias = bias_hd[:, jnp.clip(dist, 0, n)].astype(jnp.float32)
    logits = jnp.where(valid[None, :, None], logits + bias[None, None], NEG)
    m = logits.max(-1)
    p = jnp.exp(logits - m[..., None])
    s = p.sum(-1)
    o = jnp.einsum('zbhik,zbkhd->zbihd', p, vb.astype(jnp.float32))
    o = o / jnp.swapaxes(s, 2, 3)[..., None]

    def unstrided(t):
        tail = t.shape[3:]
        t = t.reshape(Bz, r, Lp, *tail)[:, :, :L]
        return jnp.moveaxis(t, 1, 2).reshape(Bz, S, *tail)

    return unstrided(o), unstrided(jnp.swapaxes(m, 2, 3)), unstrided(jnp.swapaxes(s, 2, 3))


def dilated_mixture_mixer(x, w_q, w_o, k_br, v_br, biases):
    Bz, S, _ = x.shape
    q_all = (x @ w_q).reshape(Bz, S, N_BRANCH, A_HEADS, A_HEAD_DIM)
    outs, ms, ss = [], [], []
    for g, (window, r) in enumerate(BRANCHES):
        o, m, s = dilated_branch(q_all[:, :, g], k_br[:, :, g], v_br[:, :, g], biases[g], r, window // r)
        outs.append(o); ms.append(m); ss.append(s)
    m_all = jnp.stack(ms)
    w = jnp.stack(ss) * jnp.exp(m_all - m_all.max(0))
    o = jnp.einsum('gbsh,gbshd->bshd', w, jnp.stack(outs)) / w.sum(0)[..., None]
    return o.reshape(Bz, S, A_WIDTH).astype(x.dtype) @ w_o


def setup_inputs(seed: int = 0) -> dict:
    key = jax.random.key(seed)
    ks = jax.random.split(key, 20)
    nrm = lambda k, shape, scale: jax.random.normal(k, shape, jnp.float32) * scale
    x = nrm(ks[0], (BATCH, SEQ, D_MODEL), 1.0)
    ln_g = 1.0 + nrm(ks[1], (DEPTH, 3, D_MODEL), 0.02)
    ln_b = nrm(ks[2], (DEPTH, 3, D_MODEL), 0.02)
    ffn_w_in = nrm(ks[3], (DEPTH, 2, D_MODEL, 2 * D_FF), D_MODEL ** -0.5)
    ffn_w_out = nrm(ks[4], (DEPTH, 2, D_FF, D_MODEL), BETA * D_FF ** -0.5)
    m_in_proj = nrm(ks[5], (N_A, D_MODEL, IN_PROJ_DIM), D_MODEL ** -0.5)
    m_conv_w = nrm(ks[6], (N_A, D_CONV, CONV_DIM), D_CONV ** -0.5)
    m_conv_b = nrm(ks[7], (N_A, CONV_DIM), 0.01)
    dt0 = jnp.exp(jax.random.uniform(ks[8], (N_A, M_HEADS), jnp.float32, math.log(1e-3), math.log(1e-1)))
    m_dt_bias = dt0 + jnp.log(-jnp.expm1(-dt0))
    m_a_log = jnp.log(jax.random.uniform(ks[9], (N_A, M_HEADS), jnp.float32, 1.0, 16.0))
    m_d = 1.0 + nrm(ks[10], (N_A, M_HEADS), 0.1)
    m_norm_w = 1.0 + nrm(ks[11], (N_A, D_INNER), 0.02)
    m_out_proj = nrm(ks[12], (N_A, D_INNER, D_MODEL), BETA * D_INNER ** -0.5)
    a_w_q = nrm(ks[13], (N_B, D_MODEL, Q_DIM), D_MODEL ** -0.5)
    a_w_o = nrm(ks[14], (N_B, A_WIDTH, D_MODEL), BETA * A_WIDTH ** -0.5)
    kv_k = nrm(ks[15], (D_MODEL, Q_DIM), D_MODEL ** -0.5)
    kv_v = nrm(ks[16], (D_MODEL, Q_DIM), BETA * D_MODEL ** -0.5)
    kv_w = jnp.concatenate([kv_k, kv_v], axis=1)
    rel_bias = nrm(ks[17], (NUM_BUCKETS, N_BRANCH * A_HEADS), 0.5)
    return {'x': x, 'ln_g': ln_g, 'ln_b': ln_b, 'ffn_w_in': ffn_w_in, 'ffn_w_out': ffn_w_out,
            'm_in_proj': m_in_proj, 'm_conv_w': m_conv_w, 'm_conv_b': m_conv_b,
            'm_dt_bias': m_dt_bias, 'm_a_log': m_a_log, 'm_d': m_d, 'm_norm_w': m_norm_w,
            'm_out_proj': m_out_proj, 'a_w_q': a_w_q, 'a_w_o': a_w_o, 'kv_w': kv_w,
            'rel_bias': rel_bias}


def reference(x, ln_g, ln_b, ffn_w_in, ffn_w_out, m_in_proj, m_conv_w, m_conv_b, m_dt_bias,
              m_a_log, m_d, m_norm_w, m_out_proj, a_w_q, a_w_o, kv_w, rel_bias):
    Bz, S, _ = x.shape
    k_br = v_br = biases = None
    for layer in range(DEPTH):
        x = layer_norm(ALPHA * x + FFN_RES * swiglu(x, ffn_w_in[layer, 0], ffn_w_out[layer, 0]),
                       ln_g[layer, 0], ln_b[layer, 0])
        if layer < N_A:
            h = mamba2_mixer(x, m_in_proj[layer], m_conv_w[layer], m_conv_b[layer], m_dt_bias[layer],
                             m_a_log[layer], m_d[layer], m_norm_w[layer], m_out_proj[layer])
        else:
            b = layer - N_A
            h = dilated_mixture_mixer(x, a_w_q[b], a_w_o[b], k_br, v_br, biases)
        x = layer_norm(ALPHA * x + h, ln_g[layer, 1], ln_b[layer, 1])
        x = layer_norm(ALPHA * x + FFN_RES * swiglu(x, ffn_w_in[layer, 1], ffn_w_out[layer, 1]),
                       ln_g[layer, 2], ln_b[layer, 2])
        if layer == N_A - 1:
            kv = x @ kv_w
            k_br = kv[..., :Q_DIM].reshape(Bz, S, N_BRANCH, A_HEADS, A_HEAD_DIM)
            v_br = kv[..., Q_DIM:].reshape(Bz, S, N_BRANCH, A_HEADS, A_HEAD_DIM)
            biases = []
            for g, (window, r) in enumerate(BRANCHES):
                n = window // r
                buckets = t5_causal_bucket(jnp.arange(n + 1, dtype=jnp.int32) * r)
                biases.append(rel_bias[buckets][:, g * A_HEADS:(g + 1) * A_HEADS].T)
    return x
```

```python
import contextlib
import numpy as np
import concourse.bass as bass
import concourse.mybir as mybir
from concourse.bass_utils import run_bass_kernel_spmd

F32 = mybir.dt.float32
BF16 = mybir.dt.bfloat16
AF = mybir.ActivationFunctionType
ALU = mybir.AluOpType
AX = mybir.AxisListType
ENG = ["sync", "scalar", "gpsimd", "vector", "tensor"]

D = 1024
DEPTH = 4
NT = 2048
NTILE = 16
DFF = 2816
NF = 22
ALPHA = (2.0 * DEPTH) ** 0.25
EPS = 1e-5
EPS_LN = EPS / (ALPHA * ALPHA)
DIN = 2048
NH = 32
INPROJ = 6176
ARENA = 135168


class Prog:
    EPOCH = 30000

    def __init__(self, nc):
        self.nc = nc
        self.ops = {e: [] for e in ENG}
        self.cnt = {e: 0 for e in ENG}
        self.epoch = {e: 0 for e in ENG}
        self.seen = {e: {} for e in ENG}
        self.res = {}
        self.dcnt = {}
        self.semkeys = []

    def _semkey(self, k):
        if k not in self.semkeys:
            self.semkeys.append(k)
        return k

    def op(self, eng, fn, reads=(), writes=(), dma=None, dinc=16):
        need = {}

        def add(tok):
            if tok is None:
                return
            k, v = tok
            if need.get(k, 0) < v:
                need[k] = v

        for r in reads:
            st = self.res.get(r)
            if st:
                add(st["w"])
        for w in writes:
            st = self.res.get(w)
            if st:
                add(st["w"])
                for k, v in st["r"].items():
                    add((k, v))
        waits = []
        for k, v in need.items():
            if k[0] == "dma":
                v = self.dcnt[k[1]]
            if eng == "tensor" and k[0] == "eng" and k[1] == "tensor":
                continue
            if self.seen[eng].get(k, 0) >= v:
                continue
            self.seen[eng][k] = v
            waits.append((k, v))
        if dma is not None:
            self.dcnt[dma] = self.dcnt.get(dma, 0) + dinc
            tok = (self._semkey(("dma", dma)), self.dcnt[dma])
            inc = (tok[0], dinc)
        else:
            if self.cnt[eng] >= self.EPOCH:
                self.epoch[eng] += 1
                self.cnt[eng] = 0
            self.cnt[eng] += 1
            tok = (self._semkey(("eng", eng, self.epoch[eng])), self.cnt[eng])
            inc = (tok[0], 1)
        for w in writes:
            self.res[w] = {"w": tok, "r": {}}
        for r in reads:
            st = self.res.setdefault(r, {"w": None, "r": {}})
            if st["r"].get(tok[0], 0) < tok[1]:
                st["r"][tok[0]] = tok[1]
        self.ops[eng].append((waits, fn, inc))
        return tok

    def barrier(self):
        toks = []
        for e in ENG:
            if self.cnt[e] > 0:
                toks.append((("eng", e, self.epoch[e]), self.cnt[e]))
        for dk, v in self.dcnt.items():
            if dk.startswith("cc") or dk == "agcopy":
                continue
            toks.append((("dma", dk), v))
        for e in ENG:
            waits = []
            for k, v in toks:
                if self.seen[e].get(k, 0) >= v:
                    continue
                self.seen[e][k] = v
                waits.append((k, v))
            if waits:
                self.ops[e].append((waits, None, None))

    def finish(self, eng, toks):
        waits = []
        for k, v in toks:
            if k[0] == "dma":
                v = self.dcnt[k[1]]
            waits.append((k, v))
        self.ops[eng].append((waits, None, None))

    def emit(self):
        nc = self.nc
        with contextlib.ExitStack() as st:
            sems = {}
            for k in self.semkeys:
                sems[k] = st.enter_context(nc.semaphore("s_" + "_".join(str(x) for x in k)))
            block = st.enter_context(nc.Block())

            def mk(ename):
                def body(e):
                    for waits, fn, inc in self.ops[ename]:
                        for k, v in waits:
                            e.wait_ge(sems[k], v)
                        if fn is not None:
                            ins = fn(e)
                            ins.then_inc(sems[inc[0]], inc[1])
                return body

            for ename in ENG:
                if self.ops[ename]:
                    getattr(block, ename)(mk(ename))


def _dsize(dt):
    return 2 if dt == BF16 else 4


class Builder:
    def __init__(self, upto):
        self.upto = upto
        self.only = None
        self.dbg = None
        self.nc = bass.Bass("TRN2", target_bir_lowering=False)
        self.P = Prog(self.nc)
        self.st = contextlib.ExitStack()

    def inp(self, name, shape, dt=F32):
        return self.nc.dram_tensor(name, list(shape), dt, kind="ExternalInput").ap()

    def scratch(self, name, shape, dt=F32):
        return self.nc.dram_tensor(name, list(shape), dt, kind="Internal").ap()

    def sb(self, name, shape, dt):
        return self.st.enter_context(self.nc.sbuf_tensor(name, list(shape), dt))

    def ps(self, name, shape, dt):
        return self.st.enter_context(self.nc.psum_tensor(name, list(shape), dt))

    def op(self, *a, **k):
        return self.P.op(*a, **k)

    def view(self, off, shape, dt):
        assert off % 4 == 0
        n = 1
        for s in shape[1:]:
            n *= s
        nb = n * _dsize(dt)
        assert nb % 4 == 0 and off + nb <= ARENA, (off, nb)
        self.voff = off + nb
        a = self.arena[:, off // 4:(off + nb) // 4]
        if dt != F32:
            a = a.bitcast(dt)
        if len(shape) == 3:
            a = a.rearrange("p (a b) -> p a b", a=shape[1])
        elif len(shape) == 4:
            a = a.rearrange("p (a b c) -> p a b c", a=shape[1], b=shape[2])
        if shape[0] != 128:
            a = a[0:shape[0]]
        return a

    def build(self):
        k = self
        nc, P = self.nc, self.P
        upto = self.upto
        x_d = k.inp("x", [NT, D])
        lng_d = k.inp("ln_g", [DEPTH, 3, D])
        lnb_d = k.inp("ln_b", [DEPTH, 3, D])
        consts_d = k.inp("consts", [128, 4, 128])
        out_d = nc.dram_tensor("out", [NT, D], F32, kind="ExternalOutput").ap()

        def wdecl(name, rows, cols):
            sh = k.inp(name, [rows // 8, cols]) if not getattr(self, "only", None) else k.scratch(name + "_sh", [rows // 8, cols])
            loc = k.scratch(name + "_loc", [rows // 8, cols], BF16)
            full = k.scratch(name + "_full", [rows, cols], BF16)
            return {"sh": sh, "loc": loc, "full": full, "name": name, "issued": False}

        def wgather(w, slot):
            if w["issued"] or getattr(self, "only", None):
                return
            w["issued"] = True
            sem = f"cc{slot % 6}"
            k.op("gpsimd", lambda e: e.dma_start(out=w["loc"], in_=w["sh"]), writes=[w["name"] + "_loc"], dma="agcopy")
            k.op("gpsimd", lambda e: e.collective_compute("AllGather", ALU.bypass, replica_groups=[list(range(8))],
                                                          ins=[w["loc"]], outs=[w["full"]]),
                 reads=[w["name"] + "_loc"], writes=[w["name"]], dma=sem, dinc=1)

        W_in = [[wdecl(f"ffn_w_in_{l}_{i}", 24 * 128, 8 * 256) for i in range(2)] for l in range(DEPTH)]
        W_out = [[wdecl(f"ffn_w_out_{l}_{i}", 3072, D) for i in range(2)] for l in range(DEPTH)]
        Wm_xbc = [wdecl(f"m_xbc_{l}", 32 * 128, 8 * 128) for l in range(2)]
        Wm_z = [wdecl(f"m_z_{l}", 8 * 128, 8 * 512) for l in range(2)]
        Wm_dt = [wdecl(f"m_dt_{l}", 8 * 128, 8 * NH) for l in range(2)]
        Wm_out = [wdecl(f"m_out_proj_{l}", DIN, D) for l in range(2)]
        Wq = [wdecl(f"a_w_q_{b}", 24 * 128, 8 * 128) for b in range(2)]
        Wo = [wdecl(f"a_w_o_{b}", D, D) for b in range(2)]
        Wk = wdecl("kv_k", 24 * 128, 8 * 128)
        Wv = wdecl("kv_v", 8 * 128, 8 * 512)
        btab_d = k.inp("bias_tab", [3, 128, 16, 256])
        hneg_d = k.inp("hneg", [128, 1])
        hflag_d = k.inp("hflag", [128, 8])
        KT_loc = k.scratch("KT_loc", [3072, NT], BF16)
        V_loc = k.scratch("V_loc", [NT, 3072], BF16)
        KT_all = k.scratch("KT_all", [8 * 3072, NT], BF16)
        V_all = k.scratch("V_all", [8 * NT, 3072], BF16)
        KTh = [k.scratch(f"KTh{g}", [1024, r_ * 128], BF16) for g, r_ in enumerate((1, 4, 16))]
        Vh = [k.scratch(f"Vh{g}", [r_ * 128, 1024], BF16) for g, r_ in enumerate((1, 4, 16))]
        O_d = [k.scratch(f"O_d{g}", [NT, 1056]) for g in range(3)]
        mcw_d = k.inp("m_conv_w_l", [2, 128, 32, 4])
        mcb_d = k.inp("m_conv_b_l", [2, 128, 32])
        mdtb_d = k.inp("m_dt_bias", [2, NH])
        malog_d = k.inp("m_a_log", [2, NH])
        md_d = k.inp("m_d", [2, NH])
        mnw_d = k.inp("m_norm_w", [2, DIN])
        sel1_d = k.inp("sel1", [24, 3])
        mk_d = k.inp("mk_bc", [128, 64])
        maskk_d = k.inp("maskk_bc", [128, 8])
        xs_d = k.scratch("xs_d", [NT, DIN])
        sz_d = k.scratch("sz_d", [NT, DIN])
        BT_d = k.scratch("BT_d", [1024, NT], BF16)
        CT_d = k.scratch("CT_d", [1024, NT], BF16)
        cc_in1 = k.scratch("cc_in1", [3, D])
        cc_out1 = k.scratch("cc_out1", [24, D])
        cc_in2 = k.scratch("cc_in2", [128, DIN + NH])
        cc_out2 = k.scratch("cc_out2", [1024, DIN + NH])

        xtok = k.sb("xtok", [128, NTILE, D], F32)
        gbc = k.sb("gbc", [128, D], F32)
        bbc = k.sb("bbc", [128, D], F32)
        consts = k.sb("consts_s", [128, 4, 128], F32)
        identb = k.sb("identb_s", [128, 128], BF16)
        st1 = k.sb("st1", [128, 16], F32)
        self.arena = k.sb("arena", [128, ARENA // 4], F32)
        identf, U_f, SL_f, ones_f = consts[:, 0, :], consts[:, 1, :], consts[:, 2, :], consts[:, 3, :]
        B = [k.ps(f"B{i}", [128, 512], F32) for i in range(6)]
        ptr = k.ps("B6", [128, 1024], BF16)
        B7 = k.ps("B7", [128, 512], F32)
        po = [B[4], B[5]]

        for t in range(NTILE):
            k.op("sync", lambda e, t=t: e.dma_start(out=xtok[:, t, :], in_=x_d[t * 128:(t + 1) * 128, :]),
                 writes=[f"xtok{t}"], dma="xload")
        k.op("sync", lambda e: e.dma_start(out=consts[:], in_=consts_d), writes=["consts"], dma="cload")
        k.op("vector", lambda e: e.tensor_copy(identb[:], identf), reads=["consts"], writes=["identb"])

        def load_ln(l, i):
            k.op("sync", lambda e: e.dma_start(out=gbc[:], in_=lng_d[l, i].partition_broadcast(128)),
                 writes=["gbc"], dma="lnload")
            k.op("sync", lambda e: e.dma_start(out=bbc[:], in_=lnb_d[l, i].partition_broadcast(128)),
                 writes=["bbc"], dma="lnload")

        def make_xT(t, xb16, dst, dcol, dres):
            k.op("scalar", lambda e: e.copy(xb16, xtok[:, t, :]), reads=[f"xtok{t}"], writes=["xb16"])

            def tr(e):
                ins = None
                for kc in range(8):
                    ins = e.transpose(ptr[:, kc * 128:(kc + 1) * 128], xb16[:, kc * 128:(kc + 1) * 128], identb[:])
                return ins
            k.op("tensor", tr, reads=["xb16", "identb"], writes=["B6"])
            k.op("vector", lambda e: e.tensor_copy(dst[:, :, dcol:dcol + 128],
                                                   ptr[:].rearrange("p (k t) -> p k t", k=8)),
                 reads=["B6"], writes=[dres])

        def ln_epilogue(t, scale, ytmp, ytmp2):
            xr = f"xtok{t}"
            for dh in range(2):
                k.op("vector", lambda e, dh=dh: e.scalar_tensor_tensor(
                    out=ytmp[:, dh * 512:(dh + 1) * 512], in0=po[dh][:], scalar=float(scale),
                    in1=xtok[:, t, dh * 512:(dh + 1) * 512], op0=ALU.mult, op1=ALU.add),
                    reads=[f"B{4 + dh}", xr], writes=[f"ytmp{dh}"])
            yr = ["ytmp0", "ytmp1"]
            k.op("vector", lambda e: e.tensor_reduce(out=st1[:, 0:1], in_=ytmp, axis=AX.X, op=ALU.add),
                 reads=yr, writes=["st_sum"])
            k.op("vector", lambda e: e.tensor_scalar(out=st1[:, 1:2], in0=st1[:, 0:1], scalar1=1.0 / D, scalar2=None,
                                                     op0=ALU.mult), reads=["st_sum"], writes=["st_mean"])
            k.op("vector", lambda e: e.tensor_scalar(out=ytmp, in0=ytmp, scalar1=st1[:, 1:2], scalar2=None,
                                                     op0=ALU.subtract), reads=yr + ["st_mean"], writes=yr)
            k.op("gpsimd", lambda e: e.tensor_tensor(out=ytmp2, in0=ytmp, in1=ytmp, op=ALU.mult),
                 reads=yr, writes=["ytmp2"])
            k.op("vector", lambda e: e.tensor_reduce(out=st1[:, 2:3], in_=ytmp2, axis=AX.X, op=ALU.add),
                 reads=["ytmp2"], writes=["st_ss"])
            k.op("vector", lambda e: e.tensor_scalar(out=st1[:, 3:4], in0=st1[:, 2:3], scalar1=1.0 / D,
                                                     scalar2=float(EPS_LN), op0=ALU.mult, op1=ALU.add),
                 reads=["st_ss"], writes=["st_v"])
            k.op("scalar", lambda e: e.sqrt(st1[:, 4:5], st1[:, 3:4]), reads=["st_v"], writes=["st_sd"])
            k.op("vector", lambda e: e.reciprocal(st1[:, 5:6], st1[:, 4:5]), reads=["st_sd"], writes=["st_rstd"])
            k.op("vector", lambda e: e.scalar_tensor_tensor(out=ytmp2, in0=ytmp, scalar=st1[:, 5:6], in1=gbc[:],
                                                            op0=ALU.mult, op1=ALU.mult),
                 reads=yr + ["st_rstd", "gbc"], writes=["ytmp2"])
            k.op("gpsimd", lambda e: e.tensor_tensor(out=xtok[:, t, :], in0=ytmp2, in1=bbc[:], op=ALU.add),
                 reads=["ytmp2", "bbc"], writes=[xr])

        def ffn(l, i):
            P.barrier()
            xT = k.view(0, [128, 8, 1024], BF16)
            gT = k.view(k.voff, [128, NF, 1024], BF16)
            wo = k.view(k.voff, [128, NF, D], BF16)
            wbuf = [k.view(k.voff, [128, 8, 256], BF16) for _ in range(2)]
            xb16 = k.view(k.voff, [128, D], BF16)
            ytmp = k.view(k.voff, [128, D], F32)
            ytmp2 = k.view(k.voff, [128, D], F32)
            sg = [k.view(k.voff, [128, 512], F32) for _ in range(2)]
            pg, pu = [B[0], B[1]], [B[2], B[3]]
            load_ln(l, 2 * i)
            w_in = W_in[l][i]["full"].rearrange("(j p) (kc n) -> j p kc n", p=128, kc=8)
            w_out = W_out[l][i]["full"].rearrange("(j p) d -> p j d", p=128)
            wi_r, wo_r = W_in[l][i]["name"], W_out[l][i]["name"]
            for j in range(NF):
                k.op("sync", lambda e, j=j: e.dma_start(out=wo[:, j, :], in_=w_out[:, j, :]),
                     reads=[wo_r], writes=[f"wo{j}"], dma="wo")
            cnt = 0
            for h in range(2):
                for tt in range(8):
                    make_xT(h * 8 + tt, xb16, xT, tt * 128, f"xT{tt // 4}")
                for j in range(NF):
                    b = cnt % 2
                    cnt += 1
                    k.op("sync", lambda e, j=j, b=b: e.dma_start(out=wbuf[b], in_=w_in[j]),
                         reads=[wi_r], writes=[f"wbuf{b}"], dma=f"wbuf{b}")
                    for tt in range(2):
                        pb = tt

                        def mmg(e, b=b, tt=tt, pb=pb):
                            ins = None
                            for kc in range(8):
                                ins = e.matmul(pg[pb][:], wbuf[b][:, kc, 0:128], xT[:, kc, tt * 512:(tt + 1) * 512],
                                               start=(kc == 0), stop=(kc == 7))
                            return ins

                        def mmu(e, b=b, tt=tt, pb=pb):
                            ins = None
                            for kc in range(8):
                                ins = e.matmul(pu[pb][:], wbuf[b][:, kc, 128:256], xT[:, kc, tt * 512:(tt + 1) * 512],
                                               start=(kc == 0), stop=(kc == 7))
                            return ins
                        k.op("tensor", mmg, reads=[f"wbuf{b}", f"xT{tt}"], writes=[f"B{pb}"])
                        k.op("tensor", mmu, reads=[f"wbuf{b}", f"xT{tt}"], writes=[f"B{2 + pb}"])
                        k.op("scalar", lambda e, pb=pb: e.activation(sg[pb], pg[pb][:], AF.Silu),
                             reads=[f"B{pb}"], writes=[f"sg{pb}"])
                        k.op("vector", lambda e, pb=pb, j=j, tt=tt: e.tensor_tensor(
                            out=gT[:, j, tt * 512:(tt + 1) * 512], in0=pu[pb][:], in1=sg[pb], op=ALU.mult),
                            reads=[f"B{2 + pb}", f"sg{pb}"], writes=[f"gT{j}"])
                for tt in range(8):
                    t = h * 8 + tt
                    for dh in range(2):
                        def mmo(e, tt=tt, dh=dh):
                            ins = None
                            for j in range(NF):
                                ins = e.matmul(po[dh][:], gT[:, j, tt * 128:(tt + 1) * 128],
                                               wo[:, j, dh * 512:(dh + 1) * 512], start=(j == 0), stop=(j == NF - 1))
                            return ins
                        k.op("tensor", mmo, reads=[f"gT{j}" for j in range(NF)] + [f"wo{j}" for j in range(NF)],
                             writes=[f"B{4 + dh}"])
                    ln_epilogue(t, 0.5 / ALPHA, ytmp, ytmp2)

        def mamba(L):
            P.barrier()
            wm_xbc = Wm_xbc[L]["full"].rearrange("(j p) (kc n) -> j p kc n", p=128, kc=8)
            wm_z = Wm_z[L]["full"].rearrange("(j p) (kc n) -> j p kc n", p=128, kc=8)
            wm_dt = Wm_dt[L]["full"].rearrange("(j p) (kc n) -> j p kc n", p=128, kc=8)
            wop_d = Wm_out[L]["full"].rearrange("(j p) d -> p j d", p=128)
            dt_tok = k.view(0, [128, 16, NH], F32)
            a_tok = k.view(k.voff, [128, 16, NH], F32)
            dtb_bc = k.view(k.voff, [128, NH], F32)
            A_bc = k.view(k.voff, [128, NH], F32)
            D_bc = k.view(k.voff, [128, NH], F32)
            cw = k.view(k.voff, [128, 32, 4], F32)
            cbias = k.view(k.voff, [128, 32], F32)
            Ltot = k.view(k.voff, [128, NH], F32)
            nwbc = k.view(k.voff, [128, DIN], F32)
            assert k.voff <= 16384
            k.op("sync", lambda e: e.dma_start(out=dtb_bc, in_=mdtb_d[L].partition_broadcast(128)), writes=["dtb_bc"], dma="mload")
            k.op("sync", lambda e: e.dma_start(out=A_bc, in_=malog_d[L].partition_broadcast(128)), writes=["A_bc"], dma="mload")
            k.op("sync", lambda e: e.dma_start(out=D_bc, in_=md_d[L].partition_broadcast(128)), writes=["D_bc"], dma="mload")
            k.op("sync", lambda e: e.dma_start(out=cw, in_=mcw_d[L]), writes=["cw"], dma="mload")
            k.op("sync", lambda e: e.dma_start(out=cbias, in_=mcb_d[L]), writes=["cbias"], dma="mload")
            k.op("sync", lambda e: e.dma_start(out=nwbc, in_=mnw_d[L].partition_broadcast(128)), writes=["nwbc"], dma="mload")
            k.op("scalar", lambda e: e.activation(A_bc, A_bc, AF.Exp), reads=["A_bc"], writes=["A_bc"])
            k.op("vector", lambda e: e.tensor_scalar(out=A_bc, in0=A_bc, scalar1=-1.0, scalar2=None, op0=ALU.mult),
                 reads=["A_bc"], writes=["A_bc"])
            k.op("vector", lambda e: e.memset(Ltot, 0.0), writes=["Ltot"])

            xTm = k.view(16384, [128, 8, 2052], BF16)
            wcb = [k.view(k.voff, [128, 8, 128], BF16) for _ in range(2)]
            wzb = k.view(k.voff, [128, 8, 512], BF16)
            ubuf = k.view(k.voff, [128, 2052], F32)
            acc = k.view(k.voff, [128, 2048], F32)
            cout = k.view(k.voff, [128, 2048], F32)
            cob = k.view(k.voff, [128, 2048], BF16)
            xsst = k.view(k.voff, [128, 16, 128], F32)
            szst = [k.view(k.voff, [128, 512], F32) for _ in range(2)]
            wdt = k.view(k.voff, [128, 8, NH], BF16)
            G24 = k.view(k.voff, [24, D], F32)
            sel1 = k.view(k.voff, [24, 4], F32)
            xb16 = k.view(k.voff, [128, D], BF16)
            d1 = k.view(k.voff, [128, 512], F32)
            d2 = k.view(k.voff, [128, 512], F32)
            d3 = k.view(k.voff, [128, 512], F32)

            k.op("sync", lambda e: e.dma_start(out=cc_in1, in_=xtok[125:128, 15, :]), reads=["xtok15"],
                 writes=["cc_in1"], dma="ccio")
            k.op("gpsimd", lambda e: e.collective_compute("AllGather", ALU.bypass, replica_groups=[list(range(8))],
                                                          ins=[cc_in1], outs=[cc_out1]),
                 reads=["cc_in1"], writes=["cc_out1"], dma="ccx", dinc=1)
            k.op("sync", lambda e: e.dma_start(out=G24, in_=cc_out1), reads=["cc_out1"], writes=["G24"], dma="ccio")
            k.op("sync", lambda e: e.dma_start(out=sel1[:, 0:3], in_=sel1_d), writes=["sel1"], dma="ccio")

            def halo_mm(e):
                ins = None
                for kc in range(8):
                    ins = e.matmul(B7[:, kc * 4:kc * 4 + 3], G24[:, kc * 128:(kc + 1) * 128], sel1[:, 0:3],
                                   start=True, stop=True)
                return ins
            k.op("tensor", halo_mm, reads=["G24", "sel1"], writes=["B7"])
            k.op("vector", lambda e: e.tensor_copy(xTm[:, :, 0:3], B7[:, 0:32].rearrange("p (k t) -> p k t", k=8)[:, :, 0:3]),
                 reads=["B7"], writes=["xTm_h"])
            for t in range(NTILE):
                make_xT(t, xb16, xTm, 3 + t * 128, f"xTm{t // 4}")
            xTm_all = ["xTm_h"] + [f"xTm{i}" for i in range(4)]

            k.op("gpsimd", lambda e: e.dma_start(out=wdt, in_=wm_dt[0]), reads=[Wm_dt[L]["name"]], writes=["wdt"], dma="wdt")

            def dt_mm(e):
                ins = None
                for t in range(NTILE):
                    for kc in range(8):
                        ins = e.matmul(B[5][:, t * NH:(t + 1) * NH], xTm[:, kc, 3 + t * 128:3 + (t + 1) * 128],
                                       wdt[:, kc, :], start=(kc == 0), stop=(kc == 7))
                return ins
            k.op("tensor", dt_mm, reads=["wdt"] + xTm_all, writes=["B5"])
            dtf = dt_tok.rearrange("p t h -> p (t h)")
            d1v = d1.rearrange("p (t h) -> p t h", t=16)
            k.op("vector", lambda e: e.tensor_tensor(out=d1v, in0=B[5][:].rearrange("p (t h) -> p t h", t=16),
                                                     in1=dtb_bc.unsqueeze(1).to_broadcast([128, 16, NH]), op=ALU.add),
                 reads=["B5", "dtb_bc"], writes=["d1"])
            k.op("vector", lambda e: e.tensor_scalar(out=d2, in0=d1, scalar1=-1.0, scalar2=None, op0=ALU.mult),
                 reads=["d1"], writes=["d2"])
            k.op("vector", lambda e: e.tensor_tensor(out=d2, in0=d2, in1=d1, op=ALU.max),
                 reads=["d1", "d2"], writes=["d2"])
            k.op("scalar", lambda e: e.activation(d2, d2, AF.Exp, scale=-1.0), reads=["d2"], writes=["d2"])
            k.op("scalar", lambda e: e.activation(d2, d2, AF.Ln, bias=1.0), reads=["d2"], writes=["d2"])
            k.op("vector", lambda e: e.tensor_scalar(out=d3, in0=d1, scalar1=0.0, scalar2=None, op0=ALU.max),
                 reads=["d1"], writes=["d3"])
            k.op("vector", lambda e: e.tensor_tensor(out=dtf, in0=d3, in1=d2, op=ALU.add),
                 reads=["d2", "d3"], writes=["dt_tok"])
            k.op("vector", lambda e: e.tensor_tensor(out=a_tok, in0=dt_tok, in1=A_bc.unsqueeze(1).to_broadcast([128, 16, NH]),
                                                     op=ALU.mult), reads=["dt_tok", "A_bc"], writes=["a_tok"])

            zi = 0
            for cbk in range(4):
                k.op("gpsimd", lambda e, cbk=cbk: e.dma_start(out=wzb, in_=wm_z[cbk]),
                     reads=[Wm_z[L]["name"]], writes=["wzb"], dma="wzb")
                for t in range(NTILE):
                    pb = zi % 2
                    zi += 1

                    def zmm(e, t=t, pb=pb):
                        ins = None
                        for kc in range(8):
                            ins = e.matmul(B[pb][:], xTm[:, kc, 3 + t * 128:3 + (t + 1) * 128], wzb[:, kc, :],
                                           start=(kc == 0), stop=(kc == 7))
                        return ins
                    k.op("tensor", zmm, reads=["wzb"] + xTm_all, writes=[f"B{pb}"])
                    k.op("scalar", lambda e, pb=pb: e.activation(szst[pb], B[pb][:], AF.Silu),
                         reads=[f"B{pb}"], writes=[f"szst{pb}"])
                    k.op("sync", lambda e, pb=pb, t=t, cbk=cbk: e.dma_start(
                        out=sz_d[t * 128:(t + 1) * 128, cbk * 512:(cbk + 1) * 512], in_=szst[pb]),
                        reads=[f"szst{pb}"], writes=["sz_d"], dma=f"szst{pb}")

            xs_dv = xs_d.rearrange("(t p) c -> p t c", p=128)
            for cc in range(32):
                wb = cc % 2
                k.op("gpsimd", lambda e, cc=cc, wb=wb: e.dma_start(out=wcb[wb], in_=wm_xbc[cc]),
                     reads=[Wm_xbc[L]["name"]], writes=[f"wcb{wb}"], dma=f"wcb{wb}")
                for tt in range(4):
                    def cmm(e, tt=tt, wb=wb):
                        ins = None
                        for kc in range(8):
                            ins = e.matmul(B[tt][:], wcb[wb][:, kc, :], xTm[:, kc, 3 + tt * 512:3 + (tt + 1) * 512],
                                           start=(kc == 0), stop=(kc == 7))
                        return ins
                    k.op("tensor", cmm, reads=[f"wcb{wb}"] + xTm_all, writes=[f"B{tt}"])
                    k.op("scalar", lambda e, tt=tt: e.copy(ubuf[:, 3 + tt * 512:3 + (tt + 1) * 512], B[tt][:]),
                         reads=[f"B{tt}"], writes=[f"ubuf{tt}"])

                def hmm(e, wb=wb):
                    ins = None
                    for kc in range(8):
                        ins = e.matmul(B7[:, 0:3], wcb[wb][:, kc, :], xTm[:, kc, 0:3], start=(kc == 0), stop=(kc == 7))
                    return ins
                k.op("tensor", hmm, reads=[f"wcb{wb}"] + xTm_all, writes=["B7"])
                k.op("scalar", lambda e: e.copy(ubuf[:, 0:3], B7[:, 0:3]), reads=["B7"], writes=["ubufh"])
                ub = ["ubufh"] + [f"ubuf{i}" for i in range(4)]
                ve = "vector"
                k.op(ve, lambda e, cc=cc: e.tensor_scalar(out=acc, in0=ubuf[:, 3:2051], scalar1=cw[:, cc, 3:4],
                                                          scalar2=cbias[:, cc:cc + 1], op0=ALU.mult, op1=ALU.add),
                     reads=ub + ["cw", "cbias"], writes=["acc"])
                for j in (2, 1, 0):
                    k.op(ve, lambda e, cc=cc, j=j: e.scalar_tensor_tensor(out=acc, in0=ubuf[:, j:j + 2048], scalar=cw[:, cc, j:j + 1],
                                                                          in1=acc, op0=ALU.mult, op1=ALU.add),
                         reads=ub + ["cw", "acc"], writes=["acc"])
                if cc < 16:
                    k.op("scalar", lambda e: e.activation(cout, acc, AF.Silu), reads=["acc"], writes=["cout"])
                    for tq in range(4):
                        def xtr(e, tq=tq):
                            ins = None
                            for i4 in range(4):
                                t = tq * 4 + i4
                                ins = e.transpose(B[4][:, i4 * 128:(i4 + 1) * 128], cout[:, t * 128:(t + 1) * 128], identf)
                            return ins
                        k.op("tensor", xtr, reads=["cout", "consts"], writes=["B4"])
                        k.op("vector", lambda e, tq=tq: e.tensor_copy(xsst[:, tq * 4:(tq + 1) * 4, :],
                                                                      B[4][:].rearrange("p (t c) -> p t c", t=4)),
                             reads=["B4"], writes=["xsst"])
                    k.op("sync", lambda e, cc=cc: e.dma_start(out=xs_dv[:, :, cc * 128:(cc + 1) * 128], in_=xsst),
                         reads=["xsst"], writes=["xs_d"], dma="xsst")
                else:
                    k.op("scalar", lambda e: e.activation(cob, acc, AF.Silu), reads=["acc"], writes=["cob"])
                    dst = BT_d if cc < 24 else CT_d
                    g = (cc - 16) % 8
                    k.op("sync", lambda e, dst=dst, g=g: e.dma_start(out=dst[g * 128:(g + 1) * 128, :], in_=cob),
                         reads=["cob"], writes=["BT_d" if cc < 24 else "CT_d"], dma="cob")

            P.barrier()
            xs = k.view(16384, [128, DIN], F32)
            sz = k.view(k.voff, [128, DIN], F32)
            BTc = k.view(k.voff, [128, 8, 128], BF16)
            CTc = k.view(k.voff, [128, 8, 128], BF16)
            Btok = k.view(k.voff, [128, 8, 128], BF16)
            aU = [k.view(k.voff, [128, 4, 128], F32) for _ in range(2)]
            dec = [k.view(k.voff, [128, 4, 128], F32) for _ in range(2)]
            cbm = [k.view(k.voff, [128, 128], F32) for _ in range(2)]
            MT = [k.view(k.voff, [128, 4, 128], BF16) for _ in range(2)]
            xdt = k.view(k.voff, [128, DIN], BF16)
            xdtw = k.view(k.voff, [128, DIN], BF16)
            H = k.view(k.voff, [128, DIN], F32)
            Hb = k.view(k.voff, [128, DIN], BF16)
            yg = k.view(k.voff, [128, 256], F32)
            yt = k.view(k.voff, [128, 256], F32)
            gg = k.view(k.voff, [128, 256], F32)
            gn = k.view(k.voff, [128, DIN], BF16)
            gnT = k.view(k.voff, [128, 16, 128], BF16)
            wop = k.view(k.voff, [128, 16, D], BF16)
            ytmp = k.view(k.voff, [128, D], F32)
            ytmp2 = k.view(k.voff, [128, D], F32)
            sm = k.view(k.voff, [128, 8, NH], F32)
            coef = k.view(k.voff, [128, 8, NH], F32)
            Lg = k.view(k.voff, [128, 8, NH], F32)
            mkb = k.view(k.voff, [128, 64], F32)
            mskb = k.view(k.voff, [128, 8], F32)
            ea, cd, dte, wdte, acs, tots, dtmp, ss = (sm[:, i, :] for i in range(8))

            for j in range(16):
                k.op("gpsimd", lambda e, j=j: e.dma_start(out=wop[:, j, :], in_=wop_d[:, j, :]),
                     reads=[Wm_out[L]["name"]], writes=["wop"], dma="wop")
            k.op("sync", lambda e: e.dma_start(out=mkb, in_=mk_d), writes=["mkb"], dma="mload")
            k.op("sync", lambda e: e.dma_start(out=mskb, in_=maskk_d), writes=["mskb"], dma="mload")
            load_ln(L, 1)
            BT_v = BT_d.rearrange("(g n) t -> n g t", n=128)
            CT_v = CT_d.rearrange("(g n) t -> n g t", n=128)
            xs3 = xs.rearrange("p (h q) -> p h q", h=NH)
            xdt3 = xdt.rearrange("p (h q) -> p h q", h=NH)
            xdtw3 = xdtw.rearrange("p (h q) -> p h q", h=NH)
            H3 = H.rearrange("p (h q) -> p h q", h=NH)

            def ssd_pass(full):
                k.op("vector", lambda e: e.memset(H, 0.0), writes=["H"]) if not full else None
                for c in range(NTILE):
                    csl = slice(c * 128, (c + 1) * 128)
                    k.op("sync", lambda e, csl=csl: e.dma_start(out=xs, in_=xs_d[csl, :]), reads=["xs_d"], writes=["xs"], dma="xs")
                    k.op("sync", lambda e, csl=csl: e.dma_start(out=BTc, in_=BT_v[:, :, csl]), reads=["BT_d"], writes=["BTc"], dma="BTc")
                    if full:
                        k.op("sync", lambda e, csl=csl: e.dma_start(out=CTc, in_=CT_v[:, :, csl]), reads=["CT_d"], writes=["CTc"], dma="CTc")
                        k.op("sync", lambda e, csl=csl: e.dma_start(out=sz, in_=sz_d[csl, :]), reads=["sz_d"], writes=["sz"], dma="sz")
                    def acm(e, c=c):
                        e.matmul(B7[:, 0:NH], U_f, a_tok[:, c, :], start=True, stop=True)
                        return e.matmul(B7[:, NH:2 * NH], ones_f, a_tok[:, c, :], start=True, stop=True)
                    k.op("tensor", acm, reads=["a_tok", "consts"], writes=["B7"])
                    k.op("scalar", lambda e: e.copy(sm[:, 4:6, :], B7[:, 0:2 * NH].rearrange("p (a h) -> p a h", a=2)),
                         reads=["B7"], writes=["acs"])
                    k.op("scalar", lambda e: e.activation(ea, acs, AF.Exp), reads=["acs"], writes=["ea"])
                    k.op("scalar", lambda e: e.activation(cd, tots, AF.Exp), reads=["acs"], writes=["cd"])
                    k.op("vector", lambda e: e.tensor_tensor(out=dtmp, in0=tots, in1=acs, op=ALU.subtract),
                         reads=["acs"], writes=["dtmp"])
                    k.op("scalar", lambda e: e.activation(dte, dtmp, AF.Exp), reads=["dtmp"], writes=["dte"])
                    k.op("vector", lambda e, c=c: e.tensor_tensor(out=wdte, in0=dte, in1=dt_tok[:, c, :], op=ALU.mult),
                         reads=["dte", "dt_tok"], writes=["wdte"])
                    if not full:
                        k.op("vector", lambda e: e.tensor_tensor(out=Ltot, in0=Ltot, in1=tots, op=ALU.add),
                             reads=["acs", "Ltot"], writes=["Ltot"])
                    k.op("gpsimd", lambda e: e.tensor_tensor(out=xdtw3, in0=xs3, in1=wdte.unsqueeze(2).to_broadcast([128, NH, 64]),
                                                             op=ALU.mult), reads=["xs", "wdte"], writes=["xdtw"])
                    if full:
                        k.op("vector", lambda e, c=c: e.tensor_tensor(out=xdt3, in0=xs3,
                                                                      in1=dt_tok[:, c, :].unsqueeze(2).to_broadcast([128, NH, 64]),
                                                                      op=ALU.mult), reads=["xs", "dt_tok"], writes=["xdt"])
                    def btr(e):
                        ins = None
                        for g in range(8):
                            ins = e.transpose(ptr[:, g * 128:(g + 1) * 128], BTc[:, g, :], identb[:])
                        return ins
                    k.op("tensor", btr, reads=["BTc", "identb"], writes=["B6"])
                    k.op("scalar", lambda e: e.copy(Btok, ptr[:].rearrange("p (g n) -> p g n", g=8)),
                         reads=["B6"], writes=["Btok"])
                    if full:
                        for g in range(8):
                            pb = g % 2
                            k.op("vector", lambda e, g=g, pb=pb, c=c: e.tensor_tensor(
                                out=aU[pb], in0=a_tok[:, c, 4 * g:4 * g + 4].unsqueeze(2).to_broadcast([128, 4, 128]),
                                in1=U_f.unsqueeze(1).to_broadcast([128, 4, 128]), op=ALU.mult),
                                reads=["a_tok", "consts"], writes=[f"aU{pb}"])
                            k.op("tensor", lambda e, pb=pb: e.matmul(B[2 + pb][:], SL_f, aU[pb].rearrange("p j l -> p (j l)"),
                                                                      start=True, stop=True),
                                 reads=[f"aU{pb}", "consts"], writes=[f"B{2 + pb}"])
                            k.op("scalar", lambda e, pb=pb: e.activation(dec[pb].rearrange("p j l -> p (j l)"), B[2 + pb][:], AF.Exp),
                                 reads=[f"B{2 + pb}"], writes=[f"dec{pb}"])
                            k.op("tensor", lambda e, g=g: e.matmul(B7[:, 128:256], BTc[:, g, :], CTc[:, g, :], start=True, stop=True),
                                 reads=["BTc", "CTc"], writes=["B7"])
                            k.op("vector", lambda e, pb=pb: e.tensor_tensor(out=cbm[pb], in0=B7[:, 128:256], in1=U_f, op=ALU.mult),
                                 reads=["B7", "consts"], writes=[f"cbm{pb}"])
                            k.op("vector", lambda e, pb=pb: e.tensor_tensor(out=MT[pb], in0=dec[pb],
                                                                            in1=cbm[pb].unsqueeze(1).to_broadcast([128, 4, 128]),
                                                                            op=ALU.mult),
                                 reads=[f"dec{pb}", f"cbm{pb}"], writes=[f"MT{pb}"])

                            def ymm(e, g=g, pb=pb):
                                for j in range(4):
                                    hh = 4 * g + j
                                    e.matmul(B[pb][:, j * 64:(j + 1) * 64], MT[pb][:, j, :], xdt[:, hh * 64:(hh + 1) * 64],
                                             start=True, stop=True)
                                return e.matmul(B[pb][:, 256:512], CTc[:, g, :], Hb[:, g * 256:(g + 1) * 256],
                                                start=True, stop=True)
                            k.op("tensor", ymm, reads=[f"MT{pb}", "xdt", "CTc", "Hb"], writes=[f"B{pb}"])
                            gs = slice(g * 256, (g + 1) * 256)
                            k.op("vector", lambda e, g=g, pb=pb: e.tensor_tensor(
                                out=yt.rearrange("p (j q) -> p j q", j=4), in0=B[pb][:, 256:512].rearrange("p (j q) -> p j q", j=4),
                                in1=ea[:, 4 * g:4 * g + 4].unsqueeze(2).to_broadcast([128, 4, 64]), op=ALU.mult),
                                reads=[f"B{pb}", "ea"], writes=["yt"])
                            k.op("vector", lambda e, pb=pb: e.tensor_tensor(out=yg, in0=B[pb][:, 0:256], in1=yt, op=ALU.add),
                                 reads=[f"B{pb}", "yt"], writes=["yg"])
                            k.op("gpsimd", lambda e, g=g, gs=gs: e.tensor_tensor(
                                out=yt.rearrange("p (j q) -> p j q", j=4), in0=xs[:, gs].rearrange("p (j q) -> p j q", j=4),
                                in1=D_bc[:, 4 * g:4 * g + 4].unsqueeze(2).to_broadcast([128, 4, 64]), op=ALU.mult),
                                reads=["xs", "D_bc", "yg"], writes=["yt"])
                            k.op("vector", lambda e: e.tensor_tensor(out=yg, in0=yg, in1=yt, op=ALU.add),
                                 reads=["yg", "yt"], writes=["yg"])
                            k.op("vector", lambda e, gs=gs: e.tensor_tensor(out=gg, in0=yg, in1=sz[:, gs], op=ALU.mult),
                                 reads=["yg", "sz"], writes=["gg"])
                            k.op("gpsimd", lambda e: e.tensor_tensor(out=yt, in0=gg, in1=gg, op=ALU.mult),
                                 reads=["gg"], writes=["yt"])
                            k.op("vector", lambda e: e.tensor_reduce(out=ss[:, 0:1], in_=yt, axis=AX.X, op=ALU.add),
                                 reads=["yt"], writes=["ss"])
                            k.op("vector", lambda e: e.tensor_scalar(out=ss[:, 1:2], in0=ss[:, 0:1], scalar1=1.0 / 256,
                                                                     scalar2=float(EPS), op0=ALU.mult, op1=ALU.add),
                                 reads=["ss"], writes=["ss"])
                            k.op("scalar", lambda e: e.sqrt(ss[:, 2:3], ss[:, 1:2]), reads=["ss"], writes=["ss"])
                            k.op("vector", lambda e: e.reciprocal(ss[:, 3:4], ss[:, 2:3]), reads=["ss"], writes=["ss"])
                            k.op("vector", lambda e, gs=gs: e.scalar_tensor_tensor(out=gn[:, gs], in0=gg, scalar=ss[:, 3:4],
                                                                                   in1=nwbc[:, gs], op0=ALU.mult, op1=ALU.mult),
                                 reads=["gg", "ss", "nwbc"], writes=["gn"])
                    for g2 in range(2):
                        def smm(e, g2=g2):
                            ins = None
                            for gi in range(4):
                                g = g2 * 4 + gi
                                ins = e.matmul(B[4 + (gi // 2)][:, (gi % 2) * 256:(gi % 2 + 1) * 256], Btok[:, g, :],
                                               xdtw[:, g * 256:(g + 1) * 256], start=True, stop=True)
                            return ins
                        k.op("tensor", smm, reads=["Btok", "xdtw"], writes=["B4", "B5"])
                        hsl = slice(g2 * 16, (g2 + 1) * 16)
                        k.op("vector", lambda e, hsl=hsl: e.tensor_tensor(out=H3[:, hsl, :], in0=H3[:, hsl, :],
                                                                          in1=cd[:, hsl].unsqueeze(2).to_broadcast([128, 16, 64]),
                                                                          op=ALU.mult), reads=["H", "cd", "Hb"], writes=["H"])
                        for hb in range(2):
                            cs = slice(g2 * 1024 + hb * 512, g2 * 1024 + (hb + 1) * 512)
                            k.op("vector", lambda e, cs=cs, hb=hb: e.tensor_tensor(out=H[:, cs], in0=H[:, cs], in1=B[4 + hb][:],
                                                                                  op=ALU.add),
                                 reads=["H", f"B{4 + hb}"], writes=["H"])
                    if full:
                        k.op("scalar", lambda e: e.copy(Hb, H), reads=["H"], writes=["Hb"])
                        for hf in range(2):
                            def gtr(e, hf=hf):
                                ins = None
                                for i8 in range(8):
                                    kc = hf * 8 + i8
                                    ins = e.transpose(ptr[:, i8 * 128:(i8 + 1) * 128], gn[:, kc * 128:(kc + 1) * 128], identb[:])
                                return ins
                            k.op("tensor", gtr, reads=["gn", "identb"], writes=["B6"])
                            k.op("scalar", lambda e, hf=hf: e.copy(gnT[:, hf * 8:(hf + 1) * 8, :],
                                                                   ptr[:].rearrange("p (a t) -> p a t", a=8)),
                                 reads=["B6"], writes=["gnT"])
                        for dh in range(2):
                            def omm(e, dh=dh):
                                ins = None
                                for kc in range(16):
                                    ins = e.matmul(po[dh][:], gnT[:, kc, :], wop[:, kc, dh * 512:(dh + 1) * 512],
                                                   start=(kc == 0), stop=(kc == 15))
                                return ins
                            k.op("tensor", omm, reads=["gnT", "wop"], writes=[f"B{4 + dh}"])
                        ln_epilogue(c, 1.0 / ALPHA, ytmp, ytmp2)

            ssd_pass(False)
            k.op("sync", lambda e: e.dma_start(out=cc_in2[:, 0:DIN], in_=H), reads=["H"], writes=["cc_in2"], dma="ccio")
            k.op("sync", lambda e: e.dma_start(out=cc_in2[:, DIN:DIN + NH], in_=Ltot), reads=["Ltot"], writes=["cc_in2"], dma="ccio")
            k.op("gpsimd", lambda e: e.collective_compute("AllGather", ALU.bypass, replica_groups=[list(range(8))],
                                                          ins=[cc_in2], outs=[cc_out2]),
                 reads=["cc_in2"], writes=["cc_out2"], dma="ccx", dinc=1)
            cc2v = cc_out2.rearrange("(j n) f -> n j f", n=128)
            k.op("sync", lambda e: e.dma_start(out=Lg, in_=cc2v[:, :, DIN:DIN + NH]), reads=["cc_out2"], writes=["Lg"], dma="ccio")
            k.op("vector", lambda e: e.memset(coef, 0.0), writes=["coef"])
            for j in range(8):
                for i in range(8):
                    k.op("vector", lambda e, j=j, i=i: e.scalar_tensor_tensor(out=coef[:, j, :], in0=Lg[:, i, :],
                                                                              scalar=mkb[:, j * 8 + i:j * 8 + i + 1],
                                                                              in1=coef[:, j, :], op0=ALU.mult, op1=ALU.add),
                         reads=["Lg", "mkb", "coef"], writes=["coef"])
            k.op("scalar", lambda e: e.activation(coef, coef, AF.Exp), reads=["coef"], writes=["coef"])
            k.op("vector", lambda e: e.tensor_tensor(out=coef, in0=coef, in1=mskb.unsqueeze(2).to_broadcast([128, 8, NH]),
                                                     op=ALU.mult), reads=["coef", "mskb"], writes=["coef"])
            k.op("vector", lambda e: e.memset(H, 0.0), reads=["H"], writes=["H"])
            sz3 = sz.rearrange("p (h q) -> p h q", h=NH)
            for j in range(8):
                k.op("sync", lambda e, j=j: e.dma_start(out=xs, in_=cc2v[:, j, 0:DIN]), reads=["cc_out2"], writes=["xs"], dma="xs")
                k.op("vector", lambda e, j=j: e.tensor_tensor(out=sz3, in0=xs3, in1=coef[:, j, :].unsqueeze(2).to_broadcast([128, NH, 64]),
                                                              op=ALU.mult), reads=["xs", "coef"], writes=["sz"])
                k.op("vector", lambda e: e.tensor_tensor(out=H, in0=H, in1=sz, op=ALU.add), reads=["H", "sz"], writes=["H"])
            k.op("scalar", lambda e: e.copy(Hb, H), reads=["H"], writes=["Hb"])
            ssd_pass(True)

        RS = (1, 4, 16)

        def kv_phase():
            P.barrier()
            xT = k.view(0, [128, 8, NT], BF16)
            xb16 = k.view(k.voff, [128, D], BF16)
            wk = [k.view(k.voff, [128, 8, 128], BF16) for _ in range(2)]
            wv = k.view(k.voff, [128, 8, 512], BF16)
            kst = [k.view(k.voff, [128, NT], BF16) for _ in range(2)]
            vst = [k.view(k.voff, [128, 512], BF16) for _ in range(2)]
            hfl = k.view(k.voff, [128, 8], F32)
            bl = [k.view(k.voff, [128, 1024], BF16) for _ in range(2)]
            bacc = k.view(k.voff, [128, 1024], BF16)
            for t in range(NTILE):
                make_xT(t, xb16, xT, t * 128, f"xT{t // 4}")
            xTa = [f"xT{i}" for i in range(4)]
            wkd = Wk["full"].rearrange("(j p) (kc n) -> j p kc n", p=128, kc=8)
            wvd = Wv["full"].rearrange("(j p) (kc n) -> j p kc n", p=128, kc=8)
            k.op("sync", lambda e: e.dma_start(out=hfl, in_=hflag_d), writes=["hfl"], dma="mload")
            for cch in range(24):
                g = cch // 8
                r = RS[g]
                wb = cch % 2
                k.op("gpsimd", lambda e, cch=cch, wb=wb: e.dma_start(out=wk[wb], in_=wkd[cch]),
                     reads=[Wk["name"]], writes=[f"wk{wb}"], dma=f"wk{wb}")
                kv3 = kst[wb].rearrange("p (c u) -> p c u", c=r)
                for tt in range(4):
                    def kmm(e, tt=tt, wb=wb):
                        ins = None
                        for kc in range(8):
                            ins = e.matmul(B[tt][:], wk[wb][:, kc, :], xT[:, kc, tt * 512:(tt + 1) * 512],
                                           start=(kc == 0), stop=(kc == 7))
                        return ins
                    k.op("tensor", kmm, reads=[f"wk{wb}"] + xTa, writes=[f"B{tt}"])
                    w = 512 // r
                    k.op("scalar", lambda e, tt=tt, kv3=kv3, w=w, r=r: e.copy(
                        kv3[:, :, tt * w:(tt + 1) * w], B[tt][:].rearrange("p (u c) -> p c u", c=r)),
                        reads=[f"B{tt}"], writes=[f"kst{wb}"])
                k.op("sync", lambda e, cch=cch, wb=wb: e.dma_start(out=KT_loc[cch * 128:(cch + 1) * 128, :], in_=kst[wb]),
                     reads=[f"kst{wb}"], writes=["KT_loc"], dma=f"kst{wb}")
            vi = 0
            for vb in range(6):
                k.op("gpsimd", lambda e, vb=vb: e.dma_start(out=wv, in_=wvd[vb]),
                     reads=[Wv["name"]], writes=["wv"], dma="wv")
                for t in range(NTILE):
                    pb = vi % 2
                    vi += 1

                    def vmm(e, t=t, pb=pb):
                        ins = None
                        for kc in range(8):
                            ins = e.matmul(B[pb][:], xT[:, kc, t * 128:(t + 1) * 128], wv[:, kc, :],
                                           start=(kc == 0), stop=(kc == 7))
                        return ins
                    k.op("tensor", vmm, reads=["wv"] + xTa, writes=[f"B{pb}"])
                    k.op("scalar", lambda e, pb=pb: e.copy(vst[pb], B[pb][:]), reads=[f"B{pb}"], writes=[f"vst{pb}"])
                    k.op("sync", lambda e, pb=pb, t=t, vb=vb: e.dma_start(
                        out=V_loc[t * 128:(t + 1) * 128, vb * 512:(vb + 1) * 512], in_=vst[pb]),
                        reads=[f"vst{pb}"], writes=["V_loc"], dma=f"vst{pb}")
            P.barrier()
            k.op("gpsimd", lambda e: e.collective_compute("AllGather", ALU.bypass, replica_groups=[list(range(8))],
                                                          ins=[KT_loc], outs=[KT_all]),
                 reads=["KT_loc"], writes=["KT_all"], dma="ccx", dinc=1)
            k.op("gpsimd", lambda e: e.collective_compute("AllGather", ALU.bypass, replica_groups=[list(range(8))],
                                                          ins=[V_loc], outs=[V_all]),
                 reads=["V_loc"], writes=["V_all"], dma="ccx", dinc=1)
            bi_ = 0
            for g in range(3):
                r = RS[g]
                Lc = NT // r
                for cl in range(r):
                    for kind in ("k", "v"):
                        first = True
                        for rk in (0, 1, 2, 4, 5, 6):
                            b2 = bi_ % 2
                            bi_ += 1
                            if kind == "k":
                                src = KT_all[rk * 3072 + g * 1024:rk * 3072 + (g + 1) * 1024, :].rearrange(
                                    "(hp p) t -> p hp t", p=128)[:, :, cl * Lc + Lc - 128:cl * Lc + Lc]
                                dstv = bl[b2].rearrange("p (hp t) -> p hp t", hp=8)
                            else:
                                t0 = rk * NT + cl + r * (Lc - 128)
                                src = V_all[t0:t0 + r * 127 + 1:r, g * 1024:(g + 1) * 1024]
                                dstv = bl[b2]
                            k.op("sync", lambda e, src=src, dstv=dstv: e.dma_start(out=dstv, in_=src),
                                 reads=["KT_all", "V_all"], writes=[f"bl{b2}"], dma=f"bl{b2}")
                            if first:
                                k.op("vector", lambda e, b2=b2, rk=rk: e.tensor_scalar(out=bacc, in0=bl[b2], scalar1=hfl[:, rk:rk + 1],
                                                                                       scalar2=None, op0=ALU.mult),
                                     reads=[f"bl{b2}", "hfl"], writes=["bacc"])
                                first = False
                            else:
                                k.op("vector", lambda e, b2=b2, rk=rk: e.scalar_tensor_tensor(out=bacc, in0=bl[b2], scalar=hfl[:, rk:rk + 1],
                                                                                              in1=bacc, op0=ALU.mult, op1=ALU.add),
                                     reads=[f"bl{b2}", "hfl", "bacc"], writes=["bacc"])
                        if kind == "k":
                            dst = KTh[g].rearrange("(hp p) c -> p hp c", p=128)[:, :, cl * 128:(cl + 1) * 128]
                            k.op("sync", lambda e, dst=dst: e.dma_start(out=dst, in_=bacc.rearrange("p (hp t) -> p hp t", hp=8)),
                                 reads=["bacc"], writes=["KTh"], dma="bacc")
                        else:
                            dst = Vh[g][cl * 128:(cl + 1) * 128, :]
                            k.op("sync", lambda e, dst=dst: e.dma_start(out=dst, in_=bacc), reads=["bacc"], writes=["Vh"], dma="bacc")

        def attn(b):
            layer = 2 + b
            P.barrier()
            xT = k.view(0, [128, 8, NT], BF16)
            QT = k.view(k.voff, [128, 8, NT], BF16)
            btab = k.view(k.voff, [128, 16, 256], F32)
            wq = [k.view(k.voff, [128, 8, 128], BF16) for _ in range(2)]
            KTz = [[k.view(k.voff, [128, 8, 256], BF16) for _ in range(2)] for _ in range(2)]
            Vt = [k.view(k.voff, [128, 2, 1024], BF16) for _ in range(2)]
            lb = [k.view(k.voff, [128, 2, 256], F32) for _ in range(2)]
            pp = [k.view(k.voff, [128, 2, 256], F32) for _ in range(2)]
            pT = [k.view(k.voff, [128, 4, 128], BF16) for _ in range(2)]
            ost = [k.view(k.voff, [128, 1056], F32) for _ in range(2)]
            xb16 = k.view(k.voff, [128, D], BF16)
            hneg = k.view(k.voff, [128, 1], F32)
            nmx = [k.view(k.voff, [128, 2], F32) for _ in range(2)]
            for t in range(NTILE):
                make_xT(t, xb16, xT, t * 128, f"xT{t // 4}")
            xTa = [f"xT{i}" for i in range(4)]
            for kb_ in range(2):
                for e2_ in range(2):
                    k.op("gpsimd", lambda e, kb_=kb_, e2_=e2_: e.memset(KTz[kb_][e2_], 0.0), writes=[f"KTt{kb_}"])
            k.op("sync", lambda e: e.dma_start(out=hneg, in_=hneg_d), writes=["hneg"], dma="mload")
            wqd = Wq[b]["full"].rearrange("(j p) (kc n) -> j p kc n", p=128, kc=8)
            blk = 0
            hpi = 0
            for g in range(3):
                r = RS[g]
                Lc = NT // r
                k.op("sync", lambda e, g=g: e.dma_start(out=btab, in_=btab_d[g]), writes=["btab"], dma="btab")
                for hp in range(8):
                    wb = hp % 2
                    k.op("gpsimd", lambda e, hp=hp, wb=wb, g=g: e.dma_start(
                        out=wq[wb], in_=wqd[g * 8 + hp]),
                        reads=[Wq[b]["name"]], writes=[f"wq{wb}"], dma=f"wq{wb}")
                    q3 = QT[:, hp, :].rearrange("p (c u) -> p c u", c=r)
                    for tt in range(4):
                        def qmm(e, tt=tt, wb=wb):
                            ins = None
                            for kc in range(8):
                                ins = e.matmul(B[tt][:], wq[wb][:, kc, :], xT[:, kc, tt * 512:(tt + 1) * 512],
                                               start=(kc == 0), stop=(kc == 7))
                            return ins
                        k.op("tensor", qmm, reads=[f"wq{wb}"] + xTa, writes=[f"B{tt}"])
                        w = 512 // r
                        k.op("scalar", lambda e, tt=tt, q3=q3, w=w, r=r: e.mul(
                            q3[:, :, tt * w:(tt + 1) * w], B[tt][:].rearrange("p (u c) -> p c u", c=r), 0.125),
                            reads=[f"B{tt}"], writes=["QT"])
                KTl = KT_loc[g * 1024:(g + 1) * 1024, :].rearrange("(hp p) t -> p hp t", p=128)
                KThv = KTh[g].rearrange("(hp p) c -> p hp c", p=128)
                for cl in range(r if self.dbg not in ("q", "comb") else 0):
                    for bi in range(Lc // 128):
                        kb = blk % 2
                        blk += 1
                        skip = (self.dbg or "")[2:] if (self.dbg or "").startswith("L") else ""
                        st0 = cl * Lc + bi * 128
                        tok0 = cl + r * bi * 128
                        def kload(dst_cols, src3, kb=kb):
                            for e2 in range(2):
                                ps_ = slice(e2 * 64, (e2 + 1) * 64)
                                k.op("sync", lambda e, e2=e2, ps_=ps_, kb=kb: e.dma_start(out=KTz[kb][e2][ps_, :, dst_cols], in_=src3[ps_]),
                                     reads=["KT_loc", "KTh"], writes=[f"KTt{kb}"], dma=f"KTt{kb}")
                        if bi > 0:
                            if "k" not in skip:
                                kload(slice(0, 256), KTl[:, :, st0 - 128:st0 + 128])
                            if "v" not in skip:
                                k.op("sync", lambda e, kb=kb, tok0=tok0, r=r, g=g: e.dma_start(
                                    out=Vt[kb][:, 0, :], in_=V_loc[tok0 - r * 128:tok0 - r + 1:r, g * 1024:(g + 1) * 1024]),
                                    reads=["V_loc"], writes=[f"Vt{kb}"], dma=f"Vt{kb}")
                        else:
                            if "k" not in skip:
                                kload(slice(0, 128), KThv[:, :, cl * 128:(cl + 1) * 128])
                                kload(slice(128, 256), KTl[:, :, st0:st0 + 128])
                            if "v" not in skip:
                                k.op("sync", lambda e, kb=kb, cl=cl, g=g: e.dma_start(out=Vt[kb][:, 0, :], in_=Vh[g][cl * 128:(cl + 1) * 128, :]),
                                     reads=["Vh"], writes=[f"Vt{kb}"], dma=f"Vt{kb}")
                        if "v" not in skip:
                            k.op("sync", lambda e, kb=kb, tok0=tok0, r=r, g=g: e.dma_start(
                                out=Vt[kb][:, 1, :], in_=V_loc[tok0:tok0 + r * 127 + 1:r, g * 1024:(g + 1) * 1024]),
                                reads=["V_loc"], writes=[f"Vt{kb}"], dma=f"Vt{kb}")
                        lvl = int(self.dbg[1:2]) if (self.dbg or "").startswith("L") else 9
                        for hp in range(8 if lvl >= 1 else 0):
                            pb = hpi % 2
                            hpi += 1

                            def lmm(e, hp=hp, pb=pb, kb=kb, st0=st0):
                                ins = None
                                for e2 in range(1 if "z" in (self.dbg or "") else 2):
                                    ins = e.matmul(B[pb][:, e2 * 256:(e2 + 1) * 256], QT[:, hp, st0:st0 + 128],
                                                   KTz[kb][e2][:, hp, :], start=True, stop=True)
                                return ins
                            k.op("tensor", lmm, reads=["QT", f"KTt{kb}"], writes=[f"B{pb}"])
                            k.op("vector", lambda e, hp=hp, pb=pb: e.tensor_tensor(
                                out=lb[pb], in0=B[pb][:].rearrange("p (a t) -> p a t", a=2), in1=btab[:, 2 * hp:2 * hp + 2, :], op=ALU.add),
                                reads=[f"B{pb}", "btab"], writes=[f"lb{pb}"])
                            if lvl < 2:
                                continue
                            if bi == 0:
                                k.op("vector", lambda e, pb=pb: e.tensor_scalar(out=lb[pb][:, :, 0:128], in0=lb[pb][:, :, 0:128],
                                                                                scalar1=hneg[:, 0:1], scalar2=None, op0=ALU.add),
                                     reads=[f"lb{pb}", "hneg"], writes=[f"lb{pb}"])
                            mcol = 1024 + 2 * hp
                            k.op("vector", lambda e, pb=pb, kb=kb, mcol=mcol: e.tensor_reduce(
                                out=ost[kb][:, mcol:mcol + 2], in_=lb[pb], axis=AX.X, op=ALU.max),
                                reads=[f"lb{pb}"], writes=[f"ostm{kb}"])
                            k.op("vector", lambda e, pb=pb, kb=kb, mcol=mcol: e.tensor_scalar(
                                out=nmx[pb], in0=ost[kb][:, mcol:mcol + 2], scalar1=-1.0, scalar2=None, op0=ALU.mult),
                                reads=[f"ostm{kb}"], writes=[f"nmx{pb}"])
                            if lvl < 3:
                                continue
                            for e2 in range(2):
                                k.op("scalar", lambda e, pb=pb, e2=e2: e.activation(pp[pb][:, e2, :], lb[pb][:, e2, :], AF.Exp,
                                                                                    bias=nmx[pb][:, e2:e2 + 1]),
                                     reads=[f"lb{pb}", f"nmx{pb}"], writes=[f"pp{pb}"])
                            k.op("vector", lambda e, pb=pb, kb=kb, mcol=mcol: e.tensor_reduce(
                                out=ost[kb][:, mcol + 16:mcol + 18], in_=pp[pb], axis=AX.X, op=ALU.add),
                                reads=[f"pp{pb}"], writes=[f"osts{kb}"])

                            if lvl < 4:
                                continue

                            def ptr_(e, pb=pb):
                                ins = None
                                for e2 in range(2):
                                    for kh in range(2):
                                        ins = e.transpose(B[2 + pb][:, (e2 * 2 + kh) * 128:(e2 * 2 + kh + 1) * 128],
                                                          pp[pb][:, e2, kh * 128:(kh + 1) * 128], identf)
                                return ins
                            k.op("tensor", ptr_, reads=[f"pp{pb}", "consts"], writes=[f"B{2 + pb}"])
                            k.op("scalar", lambda e, pb=pb: e.copy(pT[pb], B[2 + pb][:].rearrange("p (a t) -> p a t", a=4)),
                                 reads=[f"B{2 + pb}"], writes=[f"pT{pb}"])

                            if lvl < 5:
                                continue

                            def pvm(e, hp=hp, pb=pb, kb=kb):
                                ins = None
                                for e2 in range(2):
                                    h = 2 * hp + e2
                                    o = B[4 + h // 8][:, (h % 8) * 64:(h % 8 + 1) * 64]
                                    e.matmul(o, pT[pb][:, e2 * 2, :], Vt[kb][:, 0, h * 64:(h + 1) * 64], start=True, stop=False)
                                    ins = e.matmul(o, pT[pb][:, e2 * 2 + 1, :], Vt[kb][:, 1, h * 64:(h + 1) * 64], start=False, stop=True)
                                return ins
                            k.op("tensor", pvm, reads=[f"pT{pb}", f"Vt{kb}"], writes=[f"B{4 + hp // 4}"])
                        for dh in range(2):
                            k.op("scalar", lambda e, kb=kb, dh=dh: e.copy(ost[kb][:, dh * 512:(dh + 1) * 512], B[4 + dh][:]),
                                 reads=[f"B{4 + dh}"], writes=[f"osto{kb}"])
                        if "o" not in skip:
                          k.op("sync", lambda e, kb=kb, tok0=tok0, r=r, g=g: e.dma_start(
                            out=O_d[g][tok0:tok0 + r * 127 + 1:r, :], in_=ost[kb]),
                            reads=[f"osto{kb}", f"ostm{kb}", f"osts{kb}"], writes=[f"O_d{g}", f"ostm{kb}", f"osts{kb}", f"osto{kb}"],
                            dma=f"ost{kb}")
            P.barrier()
            Ot = [k.view(0, [128, 3, 1056], F32)]
            Ot.append(k.view(k.voff, [128, 3, 1056], F32))
            wos = k.view(k.voff, [128, 8, D], BF16)
            att = k.view(k.voff, [128, D], F32)
            tmpa = k.view(k.voff, [128, D], F32)
            a16 = k.view(k.voff, [128, D], BF16)
            aT = k.view(k.voff, [128, 8, 128], BF16)
            ytmp = k.view(k.voff, [128, D], F32)
            ytmp2 = k.view(k.voff, [128, D], F32)
            cm = k.view(k.voff, [128, 8, 16], F32)
            wod = Wo[b]["full"].rearrange("(kc p) f -> p kc f", p=128)
            for kc in range(8):
                k.op("gpsimd", lambda e, kc=kc: e.dma_start(out=wos[:, kc, :], in_=wod[:, kc, :]), reads=[Wo[b]["name"]],
                     writes=["wos"], dma="wos")
            load_ln(layer, 1)
            M_, den, rden = cm[:, 0, :], cm[:, 1, :], cm[:, 2, :]
            wg = [cm[:, 3 + g, :] for g in range(3)]
            tq = cm[:, 6, :]
            att3 = att.rearrange("p (h q) -> p h q", h=16)
            tmp3 = tmpa.rearrange("p (h q) -> p h q", h=16)
            for t in range(NTILE if self.dbg not in ("q", "blk") else 0):
                ob = t % 2
                for g in range(3):
                    k.op("sync", lambda e, g=g, ob=ob, t=t: e.dma_start(out=Ot[ob][:, g, :], in_=O_d[g][t * 128:(t + 1) * 128, :]),
                         reads=[f"O_d{g}"], writes=[f"Ot{ob}"], dma=f"Ot{ob}")
                mg = [Ot[ob][:, g, 1024:1040] for g in range(3)]
                sg_ = [Ot[ob][:, g, 1040:1056] for g in range(3)]
                k.op("vector", lambda e, mg=mg: e.tensor_tensor(out=M_, in0=mg[0], in1=mg[1], op=ALU.max), reads=[f"Ot{ob}"], writes=["cm"])
                k.op("vector", lambda e, mg=mg: e.tensor_tensor(out=M_, in0=M_, in1=mg[2], op=ALU.max), reads=[f"Ot{ob}", "cm"], writes=["cm"])
                for g in range(3):
                    k.op("vector", lambda e, g=g, mg=mg: e.tensor_tensor(out=wg[g], in0=mg[g], in1=M_, op=ALU.subtract),
                         reads=[f"Ot{ob}", "cm"], writes=["cm"])
                    k.op("scalar", lambda e, g=g: e.activation(wg[g], wg[g], AF.Exp), reads=["cm"], writes=["cm"])
                    if g == 0:
                        k.op("vector", lambda e, sg_=sg_: e.tensor_tensor(out=den, in0=wg[0], in1=sg_[0], op=ALU.mult),
                             reads=[f"Ot{ob}", "cm"], writes=["cm"])
                        k.op("vector", lambda e, ob=ob: e.tensor_tensor(
                            out=att3, in0=Ot[ob][:, 0, 0:1024].rearrange("p (h q) -> p h q", h=16),
                            in1=wg[0].unsqueeze(2).to_broadcast([128, 16, 64]), op=ALU.mult),
                            reads=[f"Ot{ob}", "cm"], writes=["att"])
                    else:
                        k.op("vector", lambda e, g=g, sg_=sg_: e.tensor_tensor(out=tq, in0=wg[g], in1=sg_[g], op=ALU.mult),
                             reads=[f"Ot{ob}", "cm"], writes=["cm"])
                        k.op("vector", lambda e: e.tensor_tensor(out=den, in0=den, in1=tq, op=ALU.add), reads=["cm"], writes=["cm"])
                        k.op("gpsimd", lambda e, g=g, ob=ob: e.tensor_tensor(
                            out=tmp3, in0=Ot[ob][:, g, 0:1024].rearrange("p (h q) -> p h q", h=16),
                            in1=wg[g].unsqueeze(2).to_broadcast([128, 16, 64]), op=ALU.mult),
                            reads=[f"Ot{ob}", "cm"], writes=["tmpa"])
                        k.op("vector", lambda e: e.tensor_tensor(out=att, in0=att, in1=tmpa, op=ALU.add),
                             reads=["att", "tmpa"], writes=["att"])
                k.op("vector", lambda e: e.reciprocal(rden, den), reads=["cm"], writes=["cm"])
                k.op("vector", lambda e: e.tensor_tensor(out=a16.rearrange("p (h q) -> p h q", h=16), in0=att3,
                                                         in1=rden.unsqueeze(2).to_broadcast([128, 16, 64]), op=ALU.mult),
                     reads=["att", "cm"], writes=["a16"])

                def atr(e):
                    ins = None
                    for kc in range(8):
                        ins = e.transpose(ptr[:, kc * 128:(kc + 1) * 128], a16[:, kc * 128:(kc + 1) * 128], identb[:])
                    return ins
                k.op("tensor", atr, reads=["a16", "identb"], writes=["B6"])
                k.op("scalar", lambda e: e.copy(aT, ptr[:].rearrange("p (a t) -> p a t", a=8)), reads=["B6"], writes=["aT"])
                for dh in range(2):
                    def omm(e, dh=dh):
                        ins = None
                        for kc in range(8):
                            ins = e.matmul(po[dh][:], aT[:, kc, :], wos[:, kc, dh * 512:(dh + 1) * 512],
                                           start=(kc == 0), stop=(kc == 7))
                        return ins
                    k.op("tensor", omm, reads=["aT", "wos"], writes=[f"B{4 + dh}"])
                ln_epilogue(t, 1.0 / ALPHA, ytmp, ytmp2)

        phases = []
        for l in range(DEPTH):
            phases.append(("ffn", l, 0))
            if l < 2:
                phases.append(("mamba", l))
            else:
                phases.append(("attn", l - 2))
            phases.append(("ffn", l, 1))
            if l == 1:
                phases.append(("kv",))
        phases = phases[:upto]
        if getattr(self, 'only', None):
            phases = list(self.only)

        def phase_weights(ph):
            if ph[0] == "ffn":
                return [W_in[ph[1]][ph[2]], W_out[ph[1]][ph[2]]]
            if ph[0] == "mamba":
                return [Wm_xbc[ph[1]], Wm_z[ph[1]], Wm_dt[ph[1]], Wm_out[ph[1]]]
            if ph[0] == "kv":
                return [Wk, Wv]
            if ph[0] == "attn":
                return [Wq[ph[1]], Wo[ph[1]]]
            return []

        for pi, ph in enumerate(phases):
            for pj in range(pi, min(pi + 3, len(phases))):
                for w in phase_weights(phases[pj]):
                    wgather(w, pj)
            if ph[0] == "ffn":
                ffn(ph[1], ph[2])
            elif ph[0] == "mamba":
                mamba(ph[1])
            elif ph[0] == "kv":
                kv_phase()
            elif ph[0] == "attn":
                attn(ph[1])

        toks = []
        for t in range(NTILE):
            toks.append(k.op("sync", lambda e, t=t: e.dma_start(out=out_d[t * 128:(t + 1) * 128, :], in_=xtok[:, t, :]),
                             reads=[f"xtok{t}"], dma="xstore"))
        P.finish("sync", toks)
        P.emit()
        self.st.close()
        return nc


def build(upto=99):
    return Builder(upto).build()


def rowshard(w, c, pad_rows=None):
    w = np.asarray(w, np.float32)
    R = w.shape[0] if pad_rows is None else pad_rows
    n = R // 8
    out = np.zeros((n, w.shape[1]), np.float32)
    lo, hi = c * n, min((c + 1) * n, w.shape[0])
    if hi > lo:
        out[:hi - lo] = w[lo:hi]
    return out


def make_consts():
    kk = np.arange(128)
    ident = np.eye(128, dtype=np.float32)
    U = (kk[:, None] <= kk[None, :]).astype(np.float32)
    SL = (kk[:, None] > kk[None, :]).astype(np.float32)
    ones = np.ones((128, 128), np.float32)
    return np.ascontiguousarray(np.stack([ident, U, SL, ones], axis=1))


def t5_bucket(dist):
    dist = np.asarray(dist, np.int64)
    max_exact = 16
    logv = np.log(np.maximum(dist, 1).astype(np.float32) / np.float32(max_exact)) / np.float32(np.log(2048 / max_exact))
    large = np.minimum(max_exact + (logv.astype(np.float32) * np.float32(16)).astype(np.int32), 31)
    return np.where(dist < max_exact, dist, large)


def make_bias_tab(rel_bias):
    rel_bias = np.asarray(rel_bias, np.float32)
    out = np.full((3, 128, 16, 256), -30000.0, np.float32)
    qi = np.arange(128)[:, None]
    kk = np.arange(256)[None, :]
    dist = 128 + qi - kk
    valid = (dist >= 0) & (dist <= 128)
    for g, r in enumerate((1, 4, 16)):
        bk = t5_bucket(np.clip(dist, 0, 128) * r)
        for h in range(16):
            vals = rel_bias[bk, g * 16 + h]
            out[g, :, h, :] = np.where(valid, vals, np.float32(-30000.0))
    return out


def tile_cols(w, col_lists, c, pad_chunks):
    w = np.asarray(w, np.float32)
    n = len(col_lists[0])
    per = pad_chunks // 8
    out = np.zeros((per * 128, 8 * n), np.float32)
    for jj in range(per):
        j = c * per + jj
        if j < len(col_lists):
            out[jj * 128:(jj + 1) * 128] = w[:, col_lists[j]].reshape(8, 128, n).transpose(1, 0, 2).reshape(128, 8 * n)
    return out


def make_in_maps(inputs, names=None):
    x = np.ascontiguousarray(inputs["x"], dtype=np.float32)
    consts = make_consts()
    cw = np.asarray(inputs["m_conv_w"], np.float32)
    cb = np.asarray(inputs["m_conv_b"], np.float32)
    cw_l = np.ascontiguousarray(cw.reshape(2, 4, 32, 128).transpose(0, 3, 2, 1))
    cb_l = np.ascontiguousarray(cb.reshape(2, 32, 128).transpose(0, 2, 1))
    ffn_cols = [np.concatenate([np.arange(j * 128, (j + 1) * 128), np.arange(DFF + j * 128, DFF + (j + 1) * 128)]) for j in range(NF)]
    maps = []
    btab = make_bias_tab(inputs["rel_bias"]) if (names is None or "bias_tab" in names) else None
    for c in range(8):
        b, s = c // 4, c % 4
        sel1 = np.zeros((24, 3), np.float32)
        if s > 0:
            for t in range(3):
                sel1[(c - 1) * 3 + t, t] = 1.0
        mk = np.zeros((8, 8), np.float32)
        msk = np.zeros((8,), np.float32)
        for j in range(8):
            if j // 4 == b and j < c:
                msk[j] = 1.0
                for i in range(8):
                    if j < i < c:
                        mk[j, i] = 1.0
        hflag = np.zeros((8,), np.float32)
        if s > 0:
            hflag[c - 1] = 1.0
        m = {
            "bias_tab": btab,
            "hneg": np.full((128, 1), 0.0 if s > 0 else -30000.0, np.float32),
            "hflag": np.ascontiguousarray(np.broadcast_to(hflag.reshape(1, 8), (128, 8))),
            "x": np.ascontiguousarray(x[b, s * NT:(s + 1) * NT, :]),
            "ln_g": np.asarray(inputs["ln_g"], np.float32),
            "ln_b": np.asarray(inputs["ln_b"], np.float32),
            "consts": consts,
            "m_conv_w_l": cw_l, "m_conv_b_l": cb_l,
            "m_dt_bias": np.asarray(inputs["m_dt_bias"], np.float32),
            "m_a_log": np.asarray(inputs["m_a_log"], np.float32),
            "m_d": np.asarray(inputs["m_d"], np.float32),
            "m_norm_w": np.asarray(inputs["m_norm_w"], np.float32),
            "sel1": sel1,
            "mk_bc": np.ascontiguousarray(np.broadcast_to(mk.reshape(1, 64), (128, 64))),
            "maskk_bc": np.ascontiguousarray(np.broadcast_to(msk.reshape(1, 8), (128, 8))),
        }
        for l in range(DEPTH):
            for i in range(2):
                m[f"ffn_w_in_{l}_{i}"] = tile_cols(inputs["ffn_w_in"][l, i], ffn_cols, c, 24)
                m[f"ffn_w_out_{l}_{i}"] = rowshard(inputs["ffn_w_out"][l, i], c, 3072)
        for l in range(2):
            m[f"m_xbc_{l}"] = tile_cols(inputs["m_in_proj"][l], [np.arange(2048 + j * 128, 2048 + (j + 1) * 128) for j in range(32)], c, 32)
            m[f"m_z_{l}"] = tile_cols(inputs["m_in_proj"][l], [np.arange(j * 512, (j + 1) * 512) for j in range(4)], c, 8)
            m[f"m_dt_{l}"] = tile_cols(inputs["m_in_proj"][l], [np.arange(6144, 6176)], c, 8)
            m[f"m_out_proj_{l}"] = rowshard(inputs["m_out_proj"][l], c)
            m[f"a_w_q_{l}"] = tile_cols(inputs["a_w_q"][l], [np.arange(j * 128, (j + 1) * 128) for j in range(24)], c, 24)
            m[f"a_w_o_{l}"] = rowshard(inputs["a_w_o"][l], c)
        m["kv_k"] = tile_cols(inputs["kv_w"], [np.arange(j * 128, (j + 1) * 128) for j in range(24)], c, 24)
        m["kv_v"] = tile_cols(inputs["kv_w"], [np.arange(3072 + j * 512, 3072 + (j + 1) * 512) for j in range(6)], c, 8)
        if names is not None:
            m = {kk: v for kk, v in m.items() if kk in names}
        maps.append(m)
    return maps


def run(inputs, upto=99, only=None, dbg=None):
    bld = Builder(upto)
    bld.only = only
    bld.dbg = dbg
    nc = bld.build()
    names = set()
    for alloc in nc.allocations:
        if isinstance(alloc, mybir.MemoryLocationSet) and alloc.kind == "ExternalInput":
            names.add(alloc.memorylocations[0].name)
    maps = make_in_maps(inputs, names)
    res = run_bass_kernel_spmd(nc, maps, core_ids=list(range(8)))
    out = np.zeros((2, 4 * NT, D), np.float32)
    for c in range(8):
        b, s = c // 4, c % 4
        out[b, s * NT:(s + 1) * NT, :] = res.results[c]["out"]
    return out


def kernel(**inputs):
    return run(inputs)
```
